# Optimizing a Trainium2 kernel written in Bass

```python
import math
import jax, jax.numpy as jnp
from jax import lax
import numpy as np

D_MODEL = 2048
BATCH = 16
SEQ = 2048
DEPTH = 1

HEAD_DIM = 64
N_SWA_HEADS = 16
N_SWA_KV = 4
N_SB_HEADS = 16
WINDOW = 128
BLOCK = 128
D_FF = 5504
EPS = 1e-6

SWA_Q = N_SWA_HEADS * HEAD_DIM
SWA_KV = N_SWA_KV * HEAD_DIM
SB_W = N_SB_HEADS * HEAD_DIM
MIX_W = SWA_Q + SB_W
IN_W = SWA_Q + 2 * SWA_KV + 3 * SB_W

kernel_name = "hybrid_swa_sink_stickbreak_macaron"


def rmsnorm(x, g):
    xf = x.astype(jnp.float32)
    y = xf * lax.rsqrt(jnp.mean(xf * xf, axis=-1, keepdims=True) + EPS)
    return (y * g.astype(jnp.float32)).astype(x.dtype)


def swiglu(h, w_gate, w_up, w_down):
    return (jax.nn.silu(h @ w_gate) * (h @ w_up)) @ w_down


def alibi_slopes(n_heads):
    i = jnp.arange(1, n_heads + 1, dtype=jnp.float32)
    return jnp.exp2(-8.0 * i / n_heads)


def swa_sink_attention(q, k, v, sinks):
    B, S, H, D = q.shape
    Hkv = k.shape[2]
    G = H // Hkv
    nb = S // BLOCK
    qb = q.reshape(B, nb, BLOCK, Hkv, G, D)
    kb = k.reshape(B, nb, BLOCK, Hkv, D)
    vb = v.reshape(B, nb, BLOCK, Hkv, D)
    pad = ((0, 0), (1, 0), (0, 0), (0, 0), (0, 0))
    kk = jnp.concatenate([jnp.pad(kb[:, :-1], pad), kb], axis=2)
    vv = jnp.concatenate([jnp.pad(vb[:, :-1], pad), vb], axis=2)
    s = jnp.einsum('bnqhgd,bnkhd->bnhgqk', qb, kk).astype(jnp.float32) * (D ** -0.5)
    q_pos = jnp.arange(BLOCK)[:, None] + BLOCK
    k_pos = jnp.arange(2 * BLOCK)[None, :]
    dist = q_pos - k_pos
    valid = (dist >= 0) & (dist < WINDOW)
    blk = jnp.arange(nb)[:, None, None]
    valid = valid[None] & ((blk > 0) | (k_pos[None] >= BLOCK))
    slopes = alibi_slopes(H).reshape(Hkv, G)
    s = s - slopes[:, :, None, None] * dist.astype(jnp.float32)
    s = jnp.where(valid[None, :, None, None], s, -jnp.inf)
    sink = jnp.broadcast_to(sinks.astype(jnp.float32).reshape(1, 1, Hkv, G, 1, 1),
                            s.shape[:-1] + (1,))
    p = jax.nn.softmax(jnp.concatenate([s, sink], axis=-1), axis=-1)[..., :-1]
    o = jnp.einsum('bnhgqk,bnkhd->bnqhgd', p.astype(v.dtype), vv)
    return o.reshape(B, S, H * D)


def stick_breaking_attention(q, k, v):
    B, S, H, D = q.shape
    nb = S // BLOCK
    qb = q.reshape(B, nb, BLOCK, H, D).transpose(1, 0, 2, 3, 4)
    k_pos = jnp.arange(S)

    def one_block(args):
        q_blk, n = args
        z = jnp.einsum('bqhd,bkhd->bhqk', q_blk, k).astype(jnp.float32) * (D ** -0.5)
        q_pos = n * BLOCK + jnp.arange(BLOCK)
        mask = k_pos[None, :] < q_pos[:, None]
        log_beta = jax.nn.log_sigmoid(z)
        log_1m = jnp.where(mask, jax.nn.log_sigmoid(-z), 0.0)
        after = lax.cumsum(log_1m, axis=3, reverse=True) - log_1m
        a = jnp.where(mask, jnp.exp(log_beta + after), 0.0)
        return jnp.einsum('bhqk,bkhd->bqhd', a.astype(v.dtype), v)

    o = lax.map(one_block, (qb, jnp.arange(nb)))
    return o.transpose(1, 0, 2, 3, 4).reshape(B, S, H * D)


def setup_inputs(seed: int = 0) -> dict:
    key = jax.random.key(seed)
    ks = jax.random.split(key, 16)
    L = DEPTH

    def w(k, shape, fan_in):
        return jax.random.normal(k, shape, jnp.float32) * (fan_in ** -0.5)

    def gain(k, shape):
        return 1.0 + 0.02 * jax.random.normal(k, shape, jnp.float32)

    return {
        "x": jax.random.normal(ks[0], (BATCH, SEQ, D_MODEL), jnp.float32),
        "ffn1_norm": gain(ks[1], (L, D_MODEL)),
        "ffn1_w_gate": w(ks[2], (L, D_MODEL, D_FF), D_MODEL),
        "ffn1_w_up": w(ks[3], (L, D_MODEL, D_FF), D_MODEL),
        "ffn1_w_down": w(ks[4], (L, D_FF, D_MODEL), D_FF),
        "mix_norm": gain(ks[5], (L, D_MODEL)),
        "w_in": w(ks[6], (L, D_MODEL, IN_W), D_MODEL),
        "swa_sinks": jax.random.normal(ks[7], (L, N_SWA_HEADS), jnp.float32),
        "swa_out_norm": gain(ks[8], (L, SWA_Q)),
        "sb_out_norm": gain(ks[9], (L, SB_W)),
        "w_out": w(ks[10], (L, MIX_W, D_MODEL), MIX_W),
        "ffn2_norm": gain(ks[11], (L, D_MODEL)),
        "ffn2_w_gate": w(ks[12], (L, D_MODEL, D_FF), D_MODEL),
        "ffn2_w_up": w(ks[13], (L, D_MODEL, D_FF), D_MODEL),
        "ffn2_w_down": w(ks[14], (L, D_FF, D_MODEL), D_FF),
        "final_norm": gain(ks[15], (D_MODEL,)),
    }


def reference(x, ffn1_norm, ffn1_w_gate, ffn1_w_up, ffn1_w_down, mix_norm, w_in,
              swa_sinks, swa_out_norm, sb_out_norm, w_out, ffn2_norm, ffn2_w_gate,
              ffn2_w_up, ffn2_w_down, final_norm):
    B, S, _ = x.shape
    for l in range(DEPTH):
        x = x + 0.5 * swiglu(rmsnorm(x, ffn1_norm[l]), ffn1_w_gate[l], ffn1_w_up[l], ffn1_w_down[l])

        h = rmsnorm(x, mix_norm[l])
        proj = h @ w_in[l]
        o1 = SWA_Q
        o2 = o1 + SWA_KV
        o3 = o2 + SWA_KV
        o4 = o3 + SB_W
        o5 = o4 + SB_W
        qa = proj[..., :o1].reshape(B, S, N_SWA_HEADS, HEAD_DIM)
        ka = proj[..., o1:o2].reshape(B, S, N_SWA_KV, HEAD_DIM)
        va = proj[..., o2:o3].reshape(B, S, N_SWA_KV, HEAD_DIM)
        qb = proj[..., o3:o4].reshape(B, S, N_SB_HEADS, HEAD_DIM)
        kb = proj[..., o4:o5].reshape(B, S, N_SB_HEADS, HEAD_DIM)
        vb = proj[..., o5:].reshape(B, S, N_SB_HEADS, HEAD_DIM)

        ya = swa_sink_attention(qa, ka, va, swa_sinks[l])
        yb = stick_breaking_attention(qb, kb, vb)

        y = jnp.concatenate([rmsnorm(ya, swa_out_norm[l]), rmsnorm(yb, sb_out_norm[l])], axis=-1)
        x = x + y @ w_out[l]

        x = x + 0.5 * swiglu(rmsnorm(x, ffn2_norm[l]), ffn2_w_gate[l], ffn2_w_up[l], ffn2_w_down[l])
    return rmsnorm(x, final_norm)
```

```python
import numpy as np
from contextlib import ExitStack
import concourse.bass as bass
import concourse.mybir as mybir
from concourse.bass_utils import run_bass_kernel_spmd

F32 = mybir.dt.float32
BF16 = mybir.dt.bfloat16
AF = mybir.ActivationFunctionType
ALU = mybir.AluOpType
P = 128
TT = 512
HD = 64
EPS = 1e-6
NEG = -30000.0


class Cfg:
    def __init__(self, D=2048, FF=5504, NSWA=16, NKV=4, NSB=16, SEQ=2048, NSEQ=2, NCORES=8):
        self.D, self.FF, self.NSWA, self.NKV, self.NSB, self.SEQ, self.NSEQ = D, FF, NSWA, NKV, NSB, SEQ, NSEQ
        self.NCORES = NCORES
        self.KC = D // P
        self.FC = FF // P
        self.SWC = NSWA * HD // P
        self.KVC = NKV * HD // P
        self.SBC = NSB * HD // P
        self.MIXC = self.SWC + self.SBC
        self.NT = SEQ // TT
        self.NBLK = SEQ // P
        self.SWV = NKV * HD
        self.SBV = NSB * HD
        self.VG = 256 if self.SBV % 256 == 0 else 128
        self.NVG = self.SBV // self.VG
        self.FA = (self.FC + 1) // 2
        self.FB = self.FC - self.FA
        self.YA0 = max(self.NSWA, self.NSB)
        self.RS = max(self.FA, self.YA0 + self.SWC)
        self.WSL = max(2 * self.KC * P, self.FA * P, self.KC * max(self.VG, self.SWV), 2 * self.MIXC * P)
        self.NQK = 2 * self.SBC + self.SWC + self.KVC


def swa_slot_to_head(cfg, u):
    c, hf = u // 2, u % 2
    g = 2 * (c // 4) + hf
    m = c % 4
    return 4 * g + m


class Res:
    __slots__ = ("name", "w", "r")

    def __init__(self, name):
        self.name = name
        self.w = None
        self.r = {}


class Op:
    __slots__ = ("eng", "fn", "deps", "sig", "sem", "val", "dma", "key", "tag")


class Sched:
    ENGS = ("pe", "act", "dve", "pool", "sp")

    def __init__(self):
        self.ops = []
        self.by_eng = {e: [] for e in self.ENGS}
        self.dma_keys = {}
        self.tag = ""

    def add(self, eng, fn, reads=(), writes=(), dma_key=None):
        op = Op()
        op.eng, op.fn, op.sig, op.sem, op.val = eng, fn, False, None, None
        op.dma = dma_key is not None
        op.tag = self.tag
        op.key = ("dma", len(self.ops)) if op.dma else eng
        deps = {}
        for r in reads:
            if r.w is not None:
                deps[id(r.w)] = r.w
        for w in writes:
            if w.w is not None:
                deps[id(w.w)] = w.w
            for o in w.r.values():
                deps[id(o)] = o
        op.deps = []
        for d in deps.values():
            if d is op:
                continue
            if (not d.dma) and d.eng == "pe" and eng == "pe" and not op.dma:
                continue
            d.sig = True
            op.deps.append(d)
        for r in reads:
            r.r[op.key] = op
        for w in writes:
            w.w = op
            w.r = {}
        if op.dma:
            op.sig = True
            op.sem = dma_key
            self.dma_keys.setdefault(dma_key, 0)
        self.ops.append(op)
        self.by_eng[eng].append(op)
        return op

    def finalize(self):
        cnt = {e: 0 for e in self.ENGS}
        dcnt = {k: 0 for k in self.dma_keys}
        for op in self.ops:
            if op.dma:
                dcnt[op.sem] += 16
                op.val = dcnt[op.sem]
            elif op.sig:
                cnt[op.eng] += 1
                op.sem = op.eng
                op.val = cnt[op.eng]
        return cnt, dcnt

    def emit(self, eng, handle, sems):
        waited = {}
        for op in self.by_eng[eng]:
            need = {}
            for d in op.deps:
                if need.get(d.sem, 0) < d.val:
                    need[d.sem] = d.val
            for s, v in need.items():
                if waited.get(s, 0) < v:
                    handle.wait_ge(sems[s], v)
                    waited[s] = v
            inst = op.fn(handle)
            if op.sig and inst is not None:
                inst.then_inc(sems[op.sem], 16 if op.dma else 1)


def build_program(cfg):
    c = cfg
    KC, FC, SWC, KVC, SBC, MIXC, NT, NBLK = c.KC, c.FC, c.SWC, c.KVC, c.SBC, c.MIXC, c.NT, c.NBLK
    D, FF = c.D, c.FF
    NTOK = c.NSEQ * c.SEQ
    nc = bass.Bass("TRN2", target_bir_lowering=False)

    def din(name, shape):
        return nc.dram_tensor(name, list(shape), F32, kind="ExternalInput").ap()

    x_d = din("x", [NTOK, D])
    wgu_d = [din("wgu1", [FC, P, 2 * KC * P]), din("wgu2", [FC, P, 2 * KC * P])]
    wd_d = [din("wd1", [KC, P, FC * P]), din("wd2", [KC, P, FC * P])]
    wqk_d = din("wqk", [c.NQK, P, KC * P])
    wvb_d = din("wvb", [c.NVG, P, KC * c.VG])
    wva_d = din("wva", [P, KC * c.SWV])
    wo_d = din("wo", [KC, P, MIXC * P])
    NVEC = 4 * KC + SWC + SBC + c.NSWA + 3
    vec_d = din("vecs", [P, NVEC])
    cf_d = din("cf32", [P, P])
    cb_d = din("cbf", [P, 5 * P])
    sbias_d = nc.dram_tensor("sbias", [c.NSWA, P, 2, 2 * P], BF16, kind="ExternalInput").ap()
    out_d = nc.dram_tensor("out", [NTOK, D], F32, kind="ExternalOutput").ap()

    S = Sched()
    es = ExitStack()

    def sb(name, shape, dt):
        return es.enter_context(nc.sbuf_tensor(name, list(shape), dt))

    xT = sb("xT", [P, KC, TT], F32)
    hT = sb("hT", [P, max(KC, MIXC), TT], BF16)
    R = sb("R", [P, c.RS, TT], BF16)
    kc_sb = sb("kcache", [P, SBC, c.SEQ], BF16)
    vc_sb = sb("vcache", [P, NBLK, c.SBV], BF16)
    ksw = sb("kswa", [P, KVC, 5 * P], BF16)
    vsw = sb("vswa", [P, 5, c.SWV], BF16)
    FS = sb("fscr", [P, 7 * TT], F32)
    _rstd = sb("rstd", [P, TT], F32)
    rstd_t = [_rstd, _rstd]
    SPt = [sb("sp%d" % i, [P, TT], BF16) for i in range(4)]
    At = [sb("at%d" % i, [P, TT], BF16) for i in range(4)]
    CRt = [sb("cr%d" % i, [P, TT], BF16) for i in range(2)]
    sg_t = [sb("sg%d" % i, [P, TT], BF16) for i in range(2)]
    sq_t = [sb("sq%d" % i, [P, TT], BF16) for i in range(2)]
    sbias_t = [sb("sbias%d" % i, [P, 2, 2 * P], BF16) for i in range(2)]
    vecs = sb("vecs_sb", [P, NVEC], F32)
    esink = sb("esink", [P, c.NSWA], F32)
    identf = sb("identf", [P, P], F32)
    cb = sb("cb", [P, 5 * P], BF16)
    NW = 4
    wring = [sb("wr%d" % i, [P, c.WSL], BF16) for i in range(NW)]
    psum = [es.enter_context(nc.psum_tensor("ps%d" % i, [P, TT], F32)) for i in range(8)]

    identb = cb[:, 0:P]
    triNeg = cb[:, P:2 * P]
    onesb = cb[:, 2 * P:3 * P]
    sel0neg = cb[:, 3 * P:4 * P]
    negmask = cb[:, 4 * P:5 * P]

    r_xT = [Res("xT%d" % i) for i in range(KC)]
    r_hT = [Res("hT%d" % i) for i in range(max(KC, MIXC))]
    r_R = [Res("R%d" % i) for i in range(c.RS)]
    r_kc = [[Res("kc") for _ in range(NT)] for _ in range(SBC)]
    r_vc = [Res("vc%d" % i) for i in range(NBLK)]
    r_ksw = [Res("ksw%d" % i) for i in range(5)]
    r_vsw = [Res("vsw%d" % i) for i in range(5)]
    r_FS = [Res("FS%d" % i) for i in range(7)]
    _r_rstd = Res("rstd")
    r_rstd = [_r_rstd, _r_rstd]
    r_SP = [Res("sp") for _ in range(4)]
    r_A = [Res("at") for _ in range(4)]
    r_CR = [Res("cr") for _ in range(2)]
    r_sg = [Res("sg") for _ in range(2)]
    r_sq = [Res("sq") for _ in range(2)]
    r_sbias = [Res("sbias") for _ in range(2)]
    r_const = Res("const")
    r_esink = Res("esink")
    r_w = [Res("wr%d" % i) for i in range(NW)]
    r_ps = [Res("ps%d" % i) for i in range(8)]
    r_out = [Res("out_dram%d" % i) for i in range(2)]

    V_G1, V_GM, V_G3, V_GF = 0, KC, 2 * KC, 3 * KC
    V_GYA = 4 * KC
    V_GYB = V_GYA + SWC
    V_SINK = V_GYB + SBC
    V_QM = V_SINK + c.NSWA
    V_E127 = V_QM + 2

    def ychunk(m):
        return R[:, c.YA0 + m, :] if m < SWC else R[:, 2 * (m - SWC), :]

    def yres(m):
        return r_R[c.YA0 + m] if m < SWC else r_R[2 * (m - SWC)]

    def mm(out, lhsT, rhs, start, stop, reads, writes):
        return S.add("pe", lambda e: e.matmul(out, lhsT=lhsT, rhs=rhs, start=start, stop=stop,
                                              skip_group_check=True), reads, writes)

    def act(out, in_, func, reads, writes, scale=None, bias=None):
        kw = {}
        if scale is not None:
            kw["scale"] = scale
        if bias is not None:
            kw["bias"] = bias
        return S.add("act", lambda e: e.activation(out, in_, func, **kw), reads, writes)

    def dve(fn, reads, writes):
        return S.add("dve", fn, reads, writes)

    def split2(L):
        n = 1
        while L // n > 2048 or L % n:
            n += 1
        return n

    wstate = {"i": 0}

    def wload(src2d, L):
        i = wstate["i"] % NW
        wstate["i"] += 1
        n = split2(L)
        dst = wring[i][:, 0:L].rearrange("p (n l) -> p n l", n=n)
        src = src2d.rearrange("p (n l) -> p n l", n=n)
        S.add("pool", lambda e: e.dma_start(out=dst, in_=src), [], [r_w[i]], dma_key="w%d" % i)
        return i

    def wload2(src3d, L):
        i = wstate["i"] % NW
        wstate["i"] += 1
        dst = wring[i][:, 0:2 * L].rearrange("p (n l) -> p n l", n=2)
        src = src3d.rearrange("n p l -> p n l")
        S.add("pool", lambda e: e.dma_start(out=dst, in_=src), [], [r_w[i]], dma_key="w%d" % i)
        return i

    pctr = {"i": 0}
    sqc = {"i": 0}

    S.add("sp", lambda e: e.dma_start(out=vecs[:], in_=vec_d), [], [r_const], dma_key="c0")
    S.add("sp", lambda e: e.dma_start(out=identf[:], in_=cf_d), [], [r_const], dma_key="c0")
    S.add("pool", lambda e: e.dma_start(out=cb[:], in_=cb_d), [], [r_const], dma_key="c1")
    act(esink[:], vecs[:, V_SINK:V_SINK + c.NSWA], AF.Exp, [r_const], [r_esink])

    def stat_chunk(src_ap, src_r, k, nchunks, width):
        j = sqc["i"] % 2
        sqc["i"] += 1
        act(sq_t[j][:], src_ap, AF.Square, [src_r], [r_sq[j]], scale=float(width) ** -0.5)
        mm(psum[7][:], onesb, sq_t[j][:], k == 0, k == nchunks - 1, [r_sq[j], r_const], [r_ps[7]])

    def stat_finish(ridx):
        rs = rstd_t[ridx]
        act(rs[:], psum[7][:], AF.Ln, [r_ps[7]], [r_rstd[ridx]], bias=EPS)
        act(rs[:], rs[:], AF.Exp, [r_rstd[ridx]], [r_rstd[ridx]], scale=-0.5)

    def rmsnorm_stats(src_chunks, src_res, nchunks, width, ridx):
        for k in range(nchunks):
            stat_chunk(src_chunks(k), src_res[k], k, nchunks, width)
        stat_finish(ridx)

    def norm_to_h(gcol, ridx, stats_done=False):
        if not stats_done:
            rmsnorm_stats(lambda k: xT[:, k, :], r_xT, KC, D, ridx)
        for k in range(KC):
            dve(lambda e, k=k: e.scalar_tensor_tensor(out=hT[:, k, :], in0=xT[:, k, :],
                                                      scalar=vecs[:, gcol + k:gcol + k + 1],
                                                      in1=rstd_t[ridx][:], op0=ALU.mult, op1=ALU.mult),
                [r_xT[k], r_rstd[ridx], r_const], [r_hT[k]])

    def ffn(li, gcol, stats_done, next_ridx):
        norm_to_h(gcol, 0, stats_done)
        for half in range(2):
            f0, nf = (0, c.FA) if half == 0 else (c.FA, c.FB)
            for fi in range(nf):
                f = f0 + fi
                w = wload(wgu_d[li][f], 2 * KC * P)
                wv = wring[w][:, 0:2 * KC * P].rearrange("p (g k j) -> p g k j", g=2, k=KC)
                bg = (2 * fi) % 4
                bu = bg + 1
                for g, bank in ((0, bg), (1, bu)):
                    for k in range(KC):
                        mm(psum[bank][:], wv[:, g, k, :], hT[:, k, :], k == 0, k == KC - 1,
                           [r_w[w], r_hT[k]], [r_ps[bank]])
                j = fi % 2
                act(sg_t[j][:], psum[bg][:], AF.Silu, [r_ps[bg]], [r_sg[j]])
                dve(lambda e, j=j, fi=fi, bu=bu: e.tensor_tensor(out=R[:, fi, :], in0=sg_t[j][:],
                                                                 in1=psum[bu][:], op=ALU.mult),
                    [r_sg[j], r_ps[bu]], [r_R[fi]])
            for cc in range(KC):
                w = wload(wd_d[li][cc][:, f0 * P:(f0 + nf) * P], nf * P)
                wv = wring[w][:, 0:nf * P].rearrange("p (f j) -> p f j", f=nf)
                bank = 4 + cc % 2
                for fi in range(nf):
                    mm(psum[bank][:], wv[:, fi, :], R[:, fi, :], fi == 0, fi == nf - 1,
                       [r_w[w], r_R[fi]], [r_ps[bank]])
                dve(lambda e, cc=cc, bank=bank: e.scalar_tensor_tensor(
                    out=xT[:, cc, :], in0=psum[bank][:], scalar=0.5, in1=xT[:, cc, :],
                    op0=ALU.mult, op1=ALU.add), [r_ps[bank], r_xT[cc]], [r_xT[cc]])
                if half == 1 and cc >= 1:
                    stat_chunk(xT[:, cc - 1, :], r_xT[cc - 1], cc - 1, KC, D)
        stat_chunk(xT[:, KC - 1, :], r_xT[KC - 1], KC - 1, KC, D)
        stat_finish(next_ridx)

    NHI = 4 if D % (4 * TT) == 0 else 1
    HWI = D // NHI
    assert HWI <= TT
    NHO = 2 if D % (2 * TT) == 0 else 1
    HWO = D // NHO
    xq_state = {}
    NQ = 4 * NHI
    CPQ = max(1, (HWI * 4) // (TT * 2))
    NHT = max(0, min(8, max(KC, MIXC) // CPQ, NQ - 3))

    def stage_loc(q):
        if q < 3 or q >= 3 + NHT:
            slot = q if q < 3 else (q - 3 - NHT) % 3
            return FS[:, slot * TT:slot * TT + HWI], [r_FS[slot]], "xin%d" % slot
        i = q - 3
        v = hT[:, i * CPQ:(i + 1) * CPQ, :].rearrange("p c t -> p (c t)").bitcast(F32)
        return v[:, 0:HWI], r_hT[i * CPQ:(i + 1) * CPQ], "xin%d" % (3 + i)

    def load_quarters(seq, T, upto):
        if seq >= c.NSEQ:
            return
        tok0 = seq * c.SEQ + T * TT
        st = xq_state.setdefault((seq, T), 0)
        while st < min(upto, NQ):
            tb, hh = st // NHI, st % NHI
            dst, res, key = stage_loc(st)
            S.add("sp", lambda e, tb=tb, hh=hh, dst=dst, tok0=tok0: e.dma_start(
                out=dst, in_=x_d[tok0 + tb * P: tok0 + (tb + 1) * P, hh * HWI:(hh + 1) * HWI]),
                [], res, dma_key=key)
            st += 1
        xq_state[(seq, T)] = st

    def do_tile(seq, T):
        if True:
            tok0 = seq * c.SEQ + T * TT
            gb0 = T * 4

            S.tag = "xin"
            load_quarters(seq, T, 3 + NHT)
            for q in range(NQ):
                tb, hh = q // NHI, q % NHI
                stg, sres, _ = stage_loc(q)
                kbase = hh * (HWI // P)
                for k0 in range(0, HWI // P, 4):
                    nk = min(4, HWI // P - k0)
                    bank = (pctr["i"]) % 4
                    pctr["i"] += 1
                    for kk in range(nk):
                        S.add("pe", lambda e, bank=bank, kk=kk, k0=k0, stg=stg: e.transpose(
                            psum[bank][:, kk * P:(kk + 1) * P], stg[:, (k0 + kk) * P:(k0 + kk + 1) * P],
                            identf[:]), sres + [r_const], [r_ps[bank]])
                    src = psum[bank][:, 0:nk * P].rearrange("p (k t) -> p k t", k=nk)
                    dst = xT[:, kbase + k0:kbase + k0 + nk, tb * P:(tb + 1) * P]
                    if q % 2 == 0:
                        dve(lambda e, dst=dst, src=src: e.tensor_copy(out=dst, in_=src),
                            [r_ps[bank]], r_xT[kbase + k0:kbase + k0 + nk])
                    else:
                        S.add("act", lambda e, dst=dst, src=src: e.copy(out=dst, in_=src),
                              [r_ps[bank]], r_xT[kbase + k0:kbase + k0 + nk])
                if q < 3:
                    load_quarters(seq, T, 3 + NHT + q + 1)
                elif q >= 3 + NHT:
                    load_quarters(seq, T, q + 4)
                if tb == 3 and hh >= 1:
                    pb = (hh - 1) * (HWI // P)
                    for kk in range(HWI // P):
                        stat_chunk(xT[:, pb + kk, :], r_xT[pb + kk], pb + kk, KC, D)
            pb = (NHI - 1) * (HWI // P)
            for kk in range(HWI // P):
                stat_chunk(xT[:, pb + kk, :], r_xT[pb + kk], pb + kk, KC, D)
            stat_finish(0)

            S.tag = "ffn1"
            ffn(0, V_G1, True, 1)

            S.tag = "proj"
            norm_to_h(V_GM, 1, True)
            if T > 0:
                S.add("dve", lambda e: e.tensor_copy(out=ksw[:, :, 0:P], in_=ksw[:, :, 4 * P:5 * P]),
                      [r_ksw[4]], [r_ksw[0]])
                S.add("dve", lambda e: e.tensor_copy(out=vsw[:, 0, :], in_=vsw[:, 4, :]),
                      [r_vsw[4]], [r_vsw[0]])

            def proj_chunk(widx, sub, bank):
                wv = wring[widx][:, 0:2 * KC * P].rearrange("p (n k j) -> p n k j", n=2, k=KC)
                for k in range(KC):
                    mm(psum[bank][:], wv[:, sub, k, :], hT[:, k, :], k == 0, k == KC - 1,
                       [r_w[widx], r_hT[k]], [r_ps[bank]])

            def evac_qpad(bank, chunk):
                for hf in range(2):
                    u = 2 * chunk + hf
                    dve(lambda e, u=u, hf=hf, bank=bank: e.tensor_scalar(
                        out=R[:, u, :], in0=psum[bank][:], scalar1=vecs[:, V_QM + hf:V_QM + hf + 1], scalar2=None,
                        op0=ALU.mult), [r_ps[bank], r_const], [r_R[u]])

            def project_qk(ch_list, handler):
                i = 0
                while i < len(ch_list):
                    ch0 = ch_list[i]
                    n = 2 if (i + 1 < len(ch_list)) else 1
                    if n == 2:
                        w = wload2(wqk_d[ch0:ch0 + 2], KC * P)
                    else:
                        w = wload(wqk_d[ch0], KC * P)
                    for sub in range(n):
                        bank = pctr["i"] % 4
                        pctr["i"] += 1
                        proj_chunk(w, sub, bank)
                        handler(ch_list[i + sub], bank)
                    i += n

            CH_SBQ, CH_SBK, CH_SWQ, CH_SWK = 0, SBC, 2 * SBC, 2 * SBC + SWC

            def h_swq(ch, bank):
                evac_qpad(bank, ch - CH_SWQ)

            def h_swk(ch, bank):
                kcid = ch - CH_SWK
                act(ksw[:, kcid, P:5 * P], psum[bank][:], AF.Copy, [r_ps[bank]], r_ksw[1:5])

            project_qk(list(range(CH_SWQ, CH_SWQ + SWC)), h_swq)
            project_qk(list(range(CH_SWK, CH_SWK + KVC)), h_swk)
            w = wload(wva_d, KC * c.SWV)
            wv = wring[w][:, 0:KC * c.SWV].rearrange("p (k n) -> p k n", k=KC)
            for tb in range(4):
                bank = pctr["i"] % 4
                pctr["i"] += 1
                for k in range(KC):
                    mm(psum[bank][:, 0:c.SWV], hT[:, k, tb * P:(tb + 1) * P], wv[:, k, :], k == 0, k == KC - 1,
                       [r_w[w], r_hT[k]], [r_ps[bank]])
                dve(lambda e, tb=tb, bank=bank: e.tensor_copy(out=vsw[:, 1 + tb, :], in_=psum[bank][:, 0:c.SWV]),
                    [r_ps[bank]], [r_vsw[1 + tb]])

            def h_sbk(ch, bank):
                j = ch - CH_SBK
                S.add("act", lambda e, j=j, bank=bank: e.copy(out=kc_sb[:, j, T * TT:(T + 1) * TT], in_=psum[bank][:]),
                      [r_ps[bank]], [r_kc[j][T]])

            project_qk(list(range(CH_SBK, CH_SBK + SBC)), h_sbk)
            for vg in range(c.NVG):
                w = wload(wvb_d[vg], KC * c.VG)
                wv = wring[w][:, 0:KC * c.VG].rearrange("p (k n) -> p k n", k=KC)
                for tb in range(4):
                    bank = pctr["i"] % 4
                    pctr["i"] += 1
                    for k in range(KC):
                        mm(psum[bank][:, 0:c.VG], hT[:, k, tb * P:(tb + 1) * P], wv[:, k, :], k == 0, k == KC - 1,
                           [r_w[w], r_hT[k]], [r_ps[bank]])
                    dve(lambda e, tb=tb, bank=bank, vg=vg: e.tensor_copy(
                        out=vc_sb[:, gb0 + tb, vg * c.VG:(vg + 1) * c.VG], in_=psum[bank][:, 0:c.VG]),
                        [r_ps[bank]], [r_vc[gb0 + tb]])

            S.tag = "swa"
            def swa_stage1(u):
                ch, hf = u // 2, u % 2
                kcid = ch // 4
                sbi = u % 2
                S.add("sp", lambda e, u=u, sbi=sbi: e.dma_start(out=sbias_t[sbi][:], in_=sbias_d[u]),
                      [], [r_sbias[sbi]], dma_key="sbias%d" % sbi)
                for bk in range(2):
                    zb = (u % 2) * 2 + bk
                    for q2 in range(2):
                        qb = bk * 2 + q2
                        first = (T == 0 and qb == 0)
                        base = q2 * 2 * P
                        started = False
                        for part in range(2):
                            if first and part == 0:
                                continue
                            kslot = qb + part
                            mm(psum[zb][:, base + part * P:base + (part + 1) * P],
                               ksw[:, kcid, kslot * P:(kslot + 1) * P], R[:, u, qb * P:(qb + 1) * P],
                               not started, True, [r_ksw[kslot], r_R[u]], [r_ps[zb]])
                            started = True
                        c0 = P if first else 0
                        for hl_ in range(2):
                            mm(psum[zb][:, base + c0:base + 2 * P], identb, sbias_t[sbi][:, hl_, c0:2 * P],
                               False, True, [r_sbias[sbi], r_const], [r_ps[zb]])
                    a0 = P if (T == 0 and bk == 0) else 0
                    pi = (u % 2) * 2 + bk
                    act(At[pi][:, a0:TT], psum[zb][:, a0:TT], AF.Exp, [r_ps[zb]], [r_A[pi]])

            def swa_stage2(u):
                ch, hf = u // 2, u % 2
                kcid = ch // 4
                lo, hi = hf * HD, (hf + 1) * HD
                ob, db = 4 + (u % 2) * 2, 5 + (u % 2) * 2
                for qb in range(4):
                    first = (T == 0 and qb == 0)
                    pi = (u % 2) * 2 + qb // 2
                    base = (qb % 2) * 2 * P
                    parts = [1] if first else [0, 1]
                    for n, part in enumerate(parts):
                        kslot = qb + part
                        mm(psum[ob][:, qb * P:(qb + 1) * P], vsw[:, kslot, kcid * P:(kcid + 1) * P],
                           At[pi][:, base + part * P:base + (part + 1) * P], n == 0, n == len(parts) - 1,
                           [r_vsw[kslot], r_A[pi]], [r_ps[ob]])
                    for n, part in enumerate(parts):
                        mm(psum[db][:, qb * P:(qb + 1) * P], onesb, At[pi][:, base + part * P:base + (part + 1) * P],
                           n == 0, n == len(parts) - 1, [r_A[pi], r_const], [r_ps[db]])
                fi = 2 + (u % 2)
                dn = FS[:, fi * TT:(fi + 1) * TT]
                act(dn, psum[db][:], AF.Ln, [r_ps[db], r_esink], [r_FS[fi]], bias=esink[:, u:u + 1])
                act(dn, dn, AF.Exp, [r_FS[fi]], [r_FS[fi]], scale=-1.0)
                dve(lambda e, ob=ob, dn=dn, lo=lo, hi=hi, ch=ch: e.tensor_tensor(
                    out=R[lo:hi, c.YA0 + ch, :], in0=psum[ob][lo:hi, :], in1=dn[lo:hi, :], op=ALU.mult),
                    [r_ps[ob], r_FS[fi]], [r_R[c.YA0 + ch]])

            swa_stage1(0)
            for u in range(c.NSWA):
                if u + 1 < c.NSWA:
                    swa_stage1(u + 1)
                swa_stage2(u)

            rmsnorm_stats(lambda k: ychunk(k), [yres(k) for k in range(SWC)], SWC, SWC * P, 0)
            for k in range(SWC):
                dve(lambda e, k=k: e.scalar_tensor_tensor(
                    out=ychunk(k), in0=ychunk(k), scalar=vecs[:, V_GYA + k:V_GYA + k + 1],
                    in1=rstd_t[0][:], op0=ALU.mult, op1=ALU.mult),
                    [yres(k), r_rstd[0], r_const], [yres(k)])

            S.tag = "sbq"
            def h_sbq(ch, bank):
                evac_qpad(bank, ch - CH_SBQ)

            n_up = min(2, SBC)
            project_qk(list(range(CH_SBQ, CH_SBQ + n_up)), h_sbq)
            weave = []

            def make_weave():
                for ch0 in range(n_up, SBC, 2):
                    n = min(2, SBC - ch0)
                    st = {}

                    def ld(ch0=ch0, n=n, st=st):
                        if n == 2:
                            st["w"] = wload2(wqk_d[CH_SBQ + ch0:CH_SBQ + ch0 + 2], KC * P)
                        else:
                            st["w"] = wload(wqk_d[CH_SBQ + ch0], KC * P)
                    weave.append((ch0, ld))
                    for sub in range(n):
                        for k in range(KC):
                            def one(sub=sub, k=k, st=st):
                                wv = wring[st["w"]][:, 0:2 * KC * P].rearrange("p (n k j) -> p n k j", n=2, k=KC)
                                mm(psum[6][:], wv[:, sub, k, :], hT[:, k, :], k == 0, k == KC - 1,
                                   [r_w[st["w"]], r_hT[k]], [r_ps[6]])
                            weave.append((ch0 + sub, one))
                        weave.append((ch0 + sub, lambda ch=ch0 + sub: evac_qpad(6, ch)))

            make_weave()
            BIG = 10 ** 6
            for cc in range(KC):
                st = {}

                def ldo(cc=cc, st=st):
                    st["w"] = wload(wo_d[cc][:, 0:SWC * P], SWC * P)
                weave.append((BIG, ldo))
                for m in range(SWC):
                    def one(cc=cc, m=m, st=st):
                        wv = wring[st["w"]][:, 0:SWC * P].rearrange("p (m j) -> p m j", m=SWC)
                        mm(psum[6][:], wv[:, m, :], ychunk(m), m == 0, m == SWC - 1,
                           [r_w[st["w"]], yres(m)], [r_ps[6]])
                    weave.append((BIG, one))

                def addx(cc=cc):
                    dve(lambda e: e.tensor_tensor(out=xT[:, cc, :], in0=psum[6][:], in1=xT[:, cc, :], op=ALU.add),
                        [r_ps[6], r_xT[cc]], [r_xT[cc]])
                weave.append((BIG, addx))
            wpos = {"i": 0}

            def weave_step(nmax=1):
                for _ in range(nmax):
                    if wpos["i"] < len(weave):
                        weave[wpos["i"]][1]()
                        wpos["i"] += 1

            def weave_flush(upto_chunk):
                while wpos["i"] < len(weave) and weave[wpos["i"]][0] <= upto_chunk:
                    weave[wpos["i"]][1]()
                    wpos["i"] += 1

            nblk_here = gb0 + 4

            def sb_head_steps(h, hl):
                j, hf = h // 2, h % 2
                zbanks = [hl * 4 + 0, hl * 4 + 1]
                cbank = 2
                obank = hl * 4 + 3
                blocks = list(range(nblk_here - 1, -1, -1))
                steps = []

                def colrange(b):
                    r = b - gb0
                    return (r * P if r > 0 else 0), TT

                def stageA(i):
                    b = blocks[i]
                    zb = zbanks[i % 2]
                    c0, c1 = colrange(b)
                    kT = kc_sb[:, j, b * P:(b + 1) * P]
                    mm(psum[zb][:, c0:c1], kT, R[:, h, c0:c1], True, True,
                       [r_kc[j][b // 4], r_R[h]], [r_ps[zb]])
                    if b >= gb0:
                        mm(psum[zb][:, c0:c0 + P], identb, negmask, False, True, [r_const], [r_ps[zb]])
                    E = FS[:, hl * TT:(hl + 1) * TT]
                    act(E[:, c0:c1], psum[zb][:, c0:c1], AF.Exp, [r_ps[zb]], [r_FS[hl]])
                    si = hl * 2 + i % 2
                    act(SPt[si][:, c0:c1], E[:, c0:c1], AF.Ln, [r_FS[hl]], [r_SP[si]], bias=1.0)

                def stageB(i):
                    b = blocks[i]
                    zb = zbanks[i % 2]
                    c0, c1 = colrange(b)
                    si = hl * 2 + i % 2
                    mm(psum[zb][:, c0:c1], triNeg, SPt[si][:, c0:c1], False, True,
                       [r_SP[si], r_const], [r_ps[zb]])
                    if i > 0:
                        p0, p1 = colrange(blocks[i - 1])
                        mm(psum[zb][:, p0:p1], sel0neg, CRt[hl][:, p0:p1], False, True,
                           [r_CR[hl], r_const], [r_ps[zb]])
                    act(At[si][:, c0:c1], psum[zb][:, c0:c1], AF.Exp, [r_ps[zb]], [r_A[si]])
                    if i < len(blocks) - 1:
                        mm(psum[cbank][:, c0:c1], onesb, SPt[si][:, c0:c1], True, True,
                           [r_SP[si], r_const], [r_ps[cbank]])
                        q0 = colrange(blocks[i - 1])[0] if i > 0 else c1
                        if q0 > c0:
                            dve(lambda e: e.tensor_copy(out=CRt[hl][:, c0:q0], in_=psum[cbank][:, c0:q0]),
                                [r_ps[cbank]], [r_CR[hl]])
                        if q0 < c1:
                            dve(lambda e: e.tensor_tensor(out=CRt[hl][:, q0:c1], in0=CRt[hl][:, q0:c1],
                                                          in1=psum[cbank][:, q0:c1], op=ALU.add),
                                [r_ps[cbank], r_CR[hl]], [r_CR[hl]])

                def stageC(i):
                    b = blocks[i]
                    c0, c1 = colrange(b)
                    si = hl * 2 + i % 2
                    mm(psum[obank][:, c0:c1], vc_sb[:, b, j * P:(j + 1) * P], At[si][:, c0:c1],
                       i == 0, i == len(blocks) - 1, [r_vc[b], r_A[si]], [r_ps[obank]])

                def fin():
                    lo, hi = hf * HD, (hf + 1) * HD
                    dve(lambda e: e.tensor_copy(out=R[lo:hi, 2 * j, :], in_=psum[obank][lo:hi, :]),
                        [r_ps[obank]], [r_R[2 * j]])

                nb = len(blocks)
                for k in range(nb + 2):
                    def cyc(k=k):
                        if k < nb:
                            stageA(k)
                        if 0 <= k - 1 < nb:
                            stageB(k - 1)
                        if 0 <= k - 2 < nb:
                            stageC(k - 2)
                    steps.append(cyc)
                steps.append(fin)
                return steps

            S.tag = "sb"
            steps_left = (c.NSB // 2) * 2 * (nblk_here + 3)
            for hp in range(c.NSB // 2):
                weave_flush(hp)
                sa = sb_head_steps(2 * hp, 0)
                sb_ = sb_head_steps(2 * hp + 1, 1)
                for a, b_ in zip(sa, sb_):
                    for fn in (a, b_):
                        fn()
                        rem = len(weave) - wpos["i"]
                        weave_step(-(-rem // max(1, steps_left)))
                        steps_left -= 1
            weave_flush(BIG)

            nseq, nT = (seq, T + 1) if T + 1 < NT else (seq + 1, 0)
            load_quarters(nseq, nT, 3)
            S.tag = "wout"
            for grp, (m0, ncn, gcol) in ((1, (SWC, SBC, V_GYB)),):
                rmsnorm_stats(lambda k, m0=m0: ychunk(m0 + k), [yres(m0 + k) for k in range(ncn)], ncn, ncn * P, grp)
                for k in range(ncn):
                    dve(lambda e, k=k, m0=m0, gcol=gcol, grp=grp: e.scalar_tensor_tensor(
                        out=ychunk(m0 + k), in0=ychunk(m0 + k), scalar=vecs[:, gcol + k:gcol + k + 1],
                        in1=rstd_t[grp][:], op0=ALU.mult, op1=ALU.mult),
                        [yres(m0 + k), r_rstd[grp], r_const], [yres(m0 + k)])
            for cc in range(KC):
                w = wload(wo_d[cc][:, SWC * P:MIXC * P], SBC * P)
                wv = wring[w][:, 0:SBC * P].rearrange("p (m j) -> p m j", m=SBC)
                bank = 4 + cc % 2
                for m in range(SBC):
                    mm(psum[bank][:], wv[:, m, :], ychunk(SWC + m), m == 0, m == SBC - 1,
                       [r_w[w], yres(SWC + m)], [r_ps[bank]])
                dve(lambda e, cc=cc, bank=bank: e.tensor_tensor(out=xT[:, cc, :], in0=psum[bank][:],
                                                                in1=xT[:, cc, :], op=ALU.add),
                    [r_ps[bank], r_xT[cc]], [r_xT[cc]])
                if cc >= 1:
                    stat_chunk(xT[:, cc - 1, :], r_xT[cc - 1], cc - 1, KC, D)
            stat_chunk(xT[:, KC - 1, :], r_xT[KC - 1], KC - 1, KC, D)
            stat_finish(0)

            S.tag = "ffn2"
            ffn(1, V_G3, True, 0)
            load_quarters(nseq, nT, 3 + NHT)

            S.tag = "fin"
            for k in range(KC):
                dve(lambda e, k=k: e.scalar_tensor_tensor(out=xT[:, k, :], in0=xT[:, k, :],
                                                          scalar=vecs[:, V_GF + k:V_GF + k + 1],
                                                          in1=rstd_t[0][:], op0=ALU.mult, op1=ALU.mult),
                    [r_xT[k], r_rstd[0], r_const], [r_xT[k]])
            hqo = (HWO + TT - 1) // TT
            for tb in range(4):
                for hh in range(NHO):
                    fres = r_FS[3 + hh * hqo:3 + (hh + 1) * hqo]
                    fo = (3 + hh * hqo) * TT
                    kbase = hh * (HWO // P)
                    for k0 in range(0, HWO // P, 4):
                        nk = min(4, HWO // P - k0)
                        bank = pctr["i"] % 4
                        pctr["i"] += 1
                        for kk in range(nk):
                            k = kbase + k0 + kk
                            S.add("pe", lambda e, bank=bank, kk=kk, k=k, tb=tb: e.transpose(
                                psum[bank][:, kk * P:(kk + 1) * P], xT[:, k, tb * P:(tb + 1) * P], identf[:]),
                                [r_xT[k], r_const], [r_ps[bank]])
                        dst = FS[:, fo + k0 * P:fo + (k0 + nk) * P]
                        src = psum[bank][:, 0:nk * P]
                        if (k0 // 4) % 2 == 0:
                            dve(lambda e, dst=dst, src=src: e.tensor_copy(out=dst, in_=src), [r_ps[bank]], fres)
                        else:
                            S.add("act", lambda e, dst=dst, src=src: e.copy(out=dst, in_=src), [r_ps[bank]], fres)
                    S.add("sp", lambda e, tb=tb, hh=hh, fo=fo: e.dma_start(
                        out=out_d[tok0 + tb * P: tok0 + (tb + 1) * P, hh * HWO:(hh + 1) * HWO], in_=FS[:, fo:fo + HWO]),
                        fres, [r_out[hh]], dma_key="out%d" % hh)

    for seq in range(c.NSEQ):
        for T in range(NT):
            do_tile(seq, T)

    S.add("sp", lambda e: None, r_out, [])

    cnt, dcnt = S.finalize()
    nc._sched = S
    sem_names = list(Sched.ENGS) + list(dcnt.keys())
    sems = {n: es.enter_context(nc.semaphore("s_" + n)) for n in sem_names}
    with nc.Block() as block:
        @block.tensor
        def _(e):
            S.emit("pe", e, sems)

        @block.scalar
        def _(e):
            S.emit("act", e, sems)

        @block.vector
        def _(e):
            S.emit("dve", e, sems)

        @block.gpsimd
        def _(e):
            S.emit("pool", e, sems)

        @block.sync
        def _(e):
            S.emit("sp", e, sems)
    es.close()
    return nc


def _stat(W, nm):
    K, N = W.shape
    kc, m = K // P, N // P
    return np.ascontiguousarray(W.reshape(kc, P, m, P).transpose(2, 1, 0, 3).reshape(m, P, kc * P))


def _mov(W):
    K, N = W.shape
    kc = K // P
    return np.ascontiguousarray(W.reshape(kc, P, N).transpose(1, 0, 2).reshape(P, kc * N))


def _vec(v):
    return np.ascontiguousarray(v.reshape(-1, P).T)


def prepare_inputs(cfg, x, ffn1_norm, ffn1_w_gate, ffn1_w_up, ffn1_w_down, mix_norm, w_in, swa_sinks,
                   swa_out_norm, sb_out_norm, w_out, ffn2_norm, ffn2_w_gate, ffn2_w_up, ffn2_w_down, final_norm):
    c = cfg
    f32 = np.float32
    sh = {}

    def gu(wg, wu):
        g = _stat(np.asarray(wg[0], f32), "g")
        u = _stat(np.asarray(wu[0], f32), "u")
        return np.ascontiguousarray(np.concatenate([g, u], axis=2))

    sh["wgu1"] = gu(ffn1_w_gate, ffn1_w_up)
    sh["wgu2"] = gu(ffn2_w_gate, ffn2_w_up)
    sh["wd1"] = _stat(np.asarray(ffn1_w_down[0], f32), "d")
    sh["wd2"] = _stat(np.asarray(ffn2_w_down[0], f32), "d")
    W = np.asarray(w_in[0], f32)
    o1 = c.NSWA * HD
    o2 = o1 + c.NKV * HD
    o3 = o2 + c.NKV * HD
    o4 = o3 + c.NSB * HD
    o5 = o4 + c.NSB * HD
    swa_heads = [swa_slot_to_head(c, u) for u in range(c.NSWA)]
    swq_cols = np.concatenate([np.arange(h * HD, (h + 1) * HD) for h in swa_heads])
    qk_cols = np.concatenate([np.arange(o3, o4), np.arange(o4, o5), swq_cols, np.arange(o1, o2)])
    sh["wqk"] = _stat(W[:, qk_cols], "qk")
    wvb = W[:, o5:]
    sh["wvb"] = np.ascontiguousarray(np.stack([_mov(wvb[:, g * c.VG:(g + 1) * c.VG]) for g in range(c.NVG)]))
    sh["wva"] = _mov(W[:, o2:o3])
    Wo = np.asarray(w_out[0], f32)
    rows = np.concatenate([swq_cols, np.arange(o1, o1 + c.NSB * HD)])
    sh["wo"] = _stat(Wo[rows, :], "o")
    gya = np.asarray(swa_out_norm[0], f32)[swq_cols]
    sinks = np.asarray(swa_sinks[0], f32)[swa_heads]
    vec = np.concatenate([_vec(np.asarray(ffn1_norm[0], f32)), _vec(np.asarray(mix_norm[0], f32)),
                          _vec(np.asarray(ffn2_norm[0], f32)), _vec(np.asarray(final_norm, f32)),
                          _vec(gya), _vec(np.asarray(sb_out_norm[0], f32)),
                          np.broadcast_to(sinks[None, :], (P, c.NSWA)),
                          np.stack([np.where(np.arange(P) < HD, HD ** -0.5, 0.0), np.where(np.arange(P) >= HD, HD ** -0.5, 0.0),
                                    (np.arange(P) == P - 1).astype(f32)], axis=1).astype(f32)], axis=1)
    sh["vecs"] = np.ascontiguousarray(vec, dtype=f32)
    sh["cf32"] = np.eye(P, dtype=f32)
    idx = np.arange(P)
    tri = -(idx[:, None] >= idx[None, :]).astype(f32)
    sel0 = np.zeros((P, P), f32)
    sel0[0, :] = -1.0
    negm = np.where(idx[:, None] < idx[None, :], 0.0, NEG).astype(f32)
    sh["cbf"] = np.ascontiguousarray(np.concatenate([np.eye(P, dtype=f32), tri, np.ones((P, P), f32), sel0, negm], axis=1))
    import ml_dtypes
    bf = ml_dtypes.bfloat16
    sb_ = np.empty((c.NSWA, P, 2, 2 * P), bf)
    kpos = np.concatenate([idx, idx + P])
    qpos = idx + P
    dist = qpos[None, :] - kpos[:, None]
    valid = (dist >= 0) & (dist < P)
    for u, h in enumerate(swa_heads):
        slope = np.float32(2.0) ** np.float32(-8.0 * (h + 1) / c.NSWA)
        b = np.where(valid, -(slope * dist.astype(f32)), NEG).astype(f32)
        b2 = np.concatenate([b[0:P, :], b[P:2 * P, :]], axis=1)
        hi_ = b2.astype(bf)
        lo_ = (b2 - hi_.astype(f32)).astype(bf)
        sb_[u, :, 0, :] = hi_
        sb_[u, :, 1, :] = lo_
    sh["sbias"] = sb_
    xs = np.asarray(x, f32).reshape(c.NCORES, c.NSEQ * c.SEQ, c.D)
    return sh, xs


_PROGRAM_CACHE = {}


def run(cfg, inputs):
    sh, xs = prepare_inputs(cfg, **inputs)
    key = (cfg.D, cfg.FF, cfg.NSWA, cfg.NKV, cfg.NSB, cfg.SEQ, cfg.NSEQ)
    nc = build_program(cfg)
    in_maps = []
    for i in range(cfg.NCORES):
        m = dict(sh)
        m["x"] = np.ascontiguousarray(xs[i])
        in_maps.append(m)
    res = run_bass_kernel_spmd(nc, in_maps, core_ids=list(range(cfg.NCORES)))
    out = np.stack([np.asarray(r["out"]) for r in res.results], axis=0)
    return out


def kernel(x, ffn1_norm, ffn1_w_gate, ffn1_w_up, ffn1_w_down, mix_norm, w_in, swa_sinks, swa_out_norm,
           sb_out_norm, w_out, ffn2_norm, ffn2_w_gate, ffn2_w_up, ffn2_w_down, final_norm):
    cfg = Cfg()
    B, Sq, Dm = x.shape
    out = run(cfg, dict(x=x, ffn1_norm=ffn1_norm, ffn1_w_gate=ffn1_w_gate, ffn1_w_up=ffn1_w_up,
                        ffn1_w_down=ffn1_w_down, mix_norm=mix_norm, w_in=w_in, swa_sinks=swa_sinks,
                        swa_out_norm=swa_out_norm, sb_out_norm=sb_out_norm, w_out=w_out, ffn2_norm=ffn2_norm,
                        ffn2_w_gate=ffn2_w_gate, ffn2_w_up=ffn2_w_up, ffn2_w_down=ffn2_w_down,
                        final_norm=final_norm))
    return out.reshape(B, Sq, Dm).astype(np.float32)
```

```python
import numpy as np
from contextlib import ExitStack
import concourse.bass as bass
import concourse.mybir as mybir
from concourse.bass_utils import run_bass_kernel_spmd

F32 = mybir.dt.float32
BF16 = mybir.dt.bfloat16
AF = mybir.ActivationFunctionType
ALU = mybir.AluOpType
P = 128
TT = 512
HD = 64
EPS = 1e-6
NEG = -30000.0


class Cfg:
    def __init__(self, D=2048, FF=5504, NSWA=16, NKV=4, NSB=16, SEQ=2048, NSEQ=2, NCORES=8):
        self.D, self.FF, self.NSWA, self.NKV, self.NSB, self.SEQ, self.NSEQ = D, FF, NSWA, NKV, NSB, SEQ, NSEQ
        self.NCORES = NCORES
        self.KC = D // P
        self.FC = FF // P
        self.SWC = NSWA * HD // P
        self.KVC = NKV * HD // P
        self.SBC = NSB * HD // P
        self.MIXC = self.SWC + self.SBC
        self.NT = SEQ // TT
        self.NBLK = SEQ // P
        self.SWV = NKV * HD
        self.SBV = NSB * HD
        self.VG = 256 if self.SBV % 256 == 0 else 128
        self.NVG = self.SBV // self.VG
        self.FA = (self.FC + 1) // 2
        self.FB = self.FC - self.FA
        self.YA0 = max(self.NSWA, self.NSB)
        self.RS = max(self.FA, self.YA0 + self.SWC)
        self.WSL = max(2 * self.KC * P, self.FA * P, self.KC * max(self.VG, self.SWV), 2 * self.MIXC * P)
        self.NQK = 2 * self.SBC + self.SWC + self.KVC


def swa_slot_to_head(cfg, u):
    c, hf = u // 2, u % 2
    g = 2 * (c // 4) + hf
    m = c % 4
    return 4 * g + m


class Res:
    __slots__ = ("name", "w", "r")

    def __init__(self, name):
        self.name = name
        self.w = None
        self.r = {}


class Op:
    __slots__ = ("eng", "fn", "deps", "sig", "sem", "val", "dma", "key", "tag")


class Sched:
    ENGS = ("pe", "act", "dve", "pool", "sp")

    def __init__(self):
        self.ops = []
        self.by_eng = {e: [] for e in self.ENGS}
        self.dma_keys = {}
        self.tag = ""

    def add(self, eng, fn, reads=(), writes=(), dma_key=None):
        op = Op()
        op.eng, op.fn, op.sig, op.sem, op.val = eng, fn, False, None, None
        op.dma = dma_key is not None
        op.tag = self.tag
        op.key = ("dma", len(self.ops)) if op.dma else eng
        deps = {}
        for r in reads:
            if r.w is not None:
                deps[id(r.w)] = r.w
        for w in writes:
            if w.w is not None:
                deps[id(w.w)] = w.w
            for o in w.r.values():
                deps[id(o)] = o
        op.deps = []
        for d in deps.values():
            if d is op:
                continue
            if (not d.dma) and d.eng == "pe" and eng == "pe" and not op.dma:
                continue
            d.sig = True
            op.deps.append(d)
        for r in reads:
            r.r[op.key] = op
        for w in writes:
            w.w = op
            w.r = {}
        if op.dma:
            op.sig = True
            op.sem = dma_key
            self.dma_keys.setdefault(dma_key, 0)
        self.ops.append(op)
        self.by_eng[eng].append(op)
        return op

    def finalize(self):
        cnt = {e: 0 for e in self.ENGS}
        dcnt = {k: 0 for k in self.dma_keys}
        for op in self.ops:
            if op.dma:
                dcnt[op.sem] += 16
                op.val = dcnt[op.sem]
            elif op.sig:
                cnt[op.eng] += 1
                op.sem = op.eng
                op.val = cnt[op.eng]
        return cnt, dcnt

    def emit(self, eng, handle, sems):
        waited = {}
        for op in self.by_eng[eng]:
            need = {}
            for d in op.deps:
                if need.get(d.sem, 0) < d.val:
                    need[d.sem] = d.val
            for s, v in need.items():
                if waited.get(s, 0) < v:
                    handle.wait_ge(sems[s], v)
                    waited[s] = v
            inst = op.fn(handle)
            if op.sig and inst is not None:
                inst.then_inc(sems[op.sem], 16 if op.dma else 1)


def build_program(cfg):
    c = cfg
    KC, FC, SWC, KVC, SBC, MIXC, NT, NBLK = c.KC, c.FC, c.SWC, c.KVC, c.SBC, c.MIXC, c.NT, c.NBLK
    D, FF = c.D, c.FF
    NTOK = c.NSEQ * c.SEQ
    nc = bass.Bass("TRN2", target_bir_lowering=False)

    def din(name, shape):
        return nc.dram_tensor(name, list(shape), F32, kind="ExternalInput").ap()

    x_d = din("x", [NTOK, D])
    wgu_d = [din("wgu1", [FC, P, 2 * KC * P]), din("wgu2", [FC, P, 2 * KC * P])]
    wd_d = [din("wd1", [KC, P, FC * P]), din("wd2", [KC, P, FC * P])]
    wqk_d = din("wqk", [c.NQK, P, KC * P])
    wvb_d = din("wvb", [c.NVG, P, KC * c.VG])
    wva_d = din("wva", [P, KC * c.SWV])
    wo_d = din("wo", [KC, P, MIXC * P])
    NVEC = 4 * KC + SWC + SBC + c.NSWA + 3
    vec_d = din("vecs", [P, NVEC])
    cf_d = din("cf32", [P, P])
    cb_d = din("cbf", [P, 5 * P])
    sbias_d = nc.dram_tensor("sbias", [c.NSWA, P, 2, 2 * P], BF16, kind="ExternalInput").ap()
    out_d = nc.dram_tensor("out", [NTOK, D], F32, kind="ExternalOutput").ap()

    S = Sched()
    es = ExitStack()

    def sb(name, shape, dt):
        return es.enter_context(nc.sbuf_tensor(name, list(shape), dt))

    xT = sb("xT", [P, KC, TT], F32)
    hT = sb("hT", [P, max(KC, MIXC), TT], BF16)
    R = sb("R", [P, c.RS, TT], BF16)
    kc_sb = sb("kcache", [P, SBC, c.SEQ], BF16)
    vc_sb = sb("vcache", [P, NBLK, c.SBV], BF16)
    ksw = sb("kswa", [P, KVC, 5 * P], BF16)
    vsw = sb("vswa", [P, 5, c.SWV], BF16)
    FS = sb("fscr", [P, 7 * TT], F32)
    _rstd = sb("rstd", [P, TT], F32)
    rstd_t = [_rstd, _rstd]
    SPt = [sb("sp%d" % i, [P, TT], BF16) for i in range(4)]
    At = [sb("at%d" % i, [P, TT], BF16) for i in range(4)]
    CRt = [sb("cr%d" % i, [P, TT], BF16) for i in range(2)]
    sg_t = [sb("sg%d" % i, [P, TT], BF16) for i in range(2)]
    sq_t = [sb("sq%d" % i, [P, TT], BF16) for i in range(2)]
    sbias_t = [sb("sbias%d" % i, [P, 2, 2 * P], BF16) for i in range(2)]
    vecs = sb("vecs_sb", [P, NVEC], F32)
    esink = sb("esink", [P, c.NSWA], F32)
    identf = sb("identf", [P, P], F32)
    cb = sb("cb", [P, 5 * P], BF16)
    NW = 4
    wring = [sb("wr%d" % i, [P, c.WSL], BF16) for i in range(NW)]
    psum = [es.enter_context(nc.psum_tensor("ps%d" % i, [P, TT], F32)) for i in range(8)]

    identb = cb[:, 0:P]
    triNeg = cb[:, P:2 * P]
    onesb = cb[:, 2 * P:3 * P]
    sel0neg = cb[:, 3 * P:4 * P]
    negmask = cb[:, 4 * P:5 * P]

    r_xT = [Res("xT%d" % i) for i in range(KC)]
    r_hT = [Res("hT%d" % i) for i in range(max(KC, MIXC))]
    r_R = [Res("R%d" % i) for i in range(c.RS)]
    r_kc = [[Res("kc") for _ in range(NT)] for _ in range(SBC)]
    r_vc = [Res("vc%d" % i) for i in range(NBLK)]
    r_ksw = [Res("ksw%d" % i) for i in range(5)]
    r_vsw = [Res("vsw%d" % i) for i in range(5)]
    r_FS = [Res("FS%d" % i) for i in range(7)]
    _r_rstd = Res("rstd")
    r_rstd = [_r_rstd, _r_rstd]
    r_SP = [Res("sp") for _ in range(4)]
    r_A = [Res("at") for _ in range(4)]
    r_CR = [Res("cr") for _ in range(2)]
    r_sg = [Res("sg") for _ in range(2)]
    r_sq = [Res("sq") for _ in range(2)]
    r_sbias = [Res("sbias") for _ in range(2)]
    r_const = Res("const")
    r_esink = Res("esink")
    r_w = [Res("wr%d" % i) for i in range(NW)]
    r_ps = [Res("ps%d" % i) for i in range(8)]
    r_out = [Res("out_dram%d" % i) for i in range(2)]

    V_G1, V_GM, V_G3, V_GF = 0, KC, 2 * KC, 3 * KC
    V_GYA = 4 * KC
    V_GYB = V_GYA + SWC
    V_SINK = V_GYB + SBC
    V_QM = V_SINK + c.NSWA
    V_E127 = V_QM + 2

    def ychunk(m):
        return R[:, c.YA0 + m, :] if m < SWC else R[:, 2 * (m - SWC), :]

    def yres(m):
        return r_R[c.YA0 + m] if m < SWC else r_R[2 * (m - SWC)]

    def mm(out, lhsT, rhs, start, stop, reads, writes):
        return S.add("pe", lambda e: e.matmul(out, lhsT=lhsT, rhs=rhs, start=start, stop=stop,
                                              skip_group_check=True), reads, writes)

    def act(out, in_, func, reads, writes, scale=None, bias=None):
        kw = {}
        if scale is not None:
            kw["scale"] = scale
        if bias is not None:
            kw["bias"] = bias
        return S.add("act", lambda e: e.activation(out, in_, func, **kw), reads, writes)

    def dve(fn, reads, writes):
        return S.add("dve", fn, reads, writes)

    def split2(L):
        n = 1
        while L // n > 2048 or L % n:
            n += 1
        return n

    wstate = {"i": 0}

    def wload(src2d, L):
        i = wstate["i"] % NW
        wstate["i"] += 1
        n = split2(L)
        dst = wring[i][:, 0:L].rearrange("p (n l) -> p n l", n=n)
        src = src2d.rearrange("p (n l) -> p n l", n=n)
        S.add("pool", lambda e: e.dma_start(out=dst, in_=src), [], [r_w[i]], dma_key="w%d" % i)
        return i

    def wload2(src3d, L):
        i = wstate["i"] % NW
        wstate["i"] += 1
        dst = wring[i][:, 0:2 * L].rearrange("p (n l) -> p n l", n=2)
        src = src3d.rearrange("n p l -> p n l")
        S.add("pool", lambda e: e.dma_start(out=dst, in_=src), [], [r_w[i]], dma_key="w%d" % i)
        return i

    pctr = {"i": 0}
    sqc = {"i": 0}

    S.add("sp", lambda e: e.dma_start(out=vecs[:], in_=vec_d), [], [r_const], dma_key="c0")
    S.add("sp", lambda e: e.dma_start(out=identf[:], in_=cf_d), [], [r_const], dma_key="c0")
    S.add("pool", lambda e: e.dma_start(out=cb[:], in_=cb_d), [], [r_const], dma_key="c1")
    act(esink[:], vecs[:, V_SINK:V_SINK + c.NSWA], AF.Exp, [r_const], [r_esink])

    def stat_chunk(src_ap, src_r, k, nchunks, width):
        j = sqc["i"] % 2
        sqc["i"] += 1
        act(sq_t[j][:], src_ap, AF.Square, [src_r], [r_sq[j]], scale=float(width) ** -0.5)
        mm(psum[7][:], onesb, sq_t[j][:], k == 0, k == nchunks - 1, [r_sq[j], r_const], [r_ps[7]])

    def stat_finish(ridx):
        rs = rstd_t[ridx]
        act(rs[:], psum[7][:], AF.Ln, [r_ps[7]], [r_rstd[ridx]], bias=EPS)
        act(rs[:], rs[:], AF.Exp, [r_rstd[ridx]], [r_rstd[ridx]], scale=-0.5)

    def rmsnorm_stats(src_chunks, src_res, nchunks, width, ridx):
        for k in range(nchunks):
            stat_chunk(src_chunks(k), src_res[k], k, nchunks, width)
        stat_finish(ridx)

    def norm_to_h(gcol, ridx, stats_done=False):
        if not stats_done:
            rmsnorm_stats(lambda k: xT[:, k, :], r_xT, KC, D, ridx)
        for k in range(KC):
            dve(lambda e, k=k: e.scalar_tensor_tensor(out=hT[:, k, :], in0=xT[:, k, :],
                                                      scalar=vecs[:, gcol + k:gcol + k + 1],
                                                      in1=rstd_t[ridx][:], op0=ALU.mult, op1=ALU.mult),
                [r_xT[k], r_rstd[ridx], r_const], [r_hT[k]])

    def ffn(li, gcol, stats_done, next_ridx):
        norm_to_h(gcol, 0, stats_done)
        for half in range(2):
            f0, nf = (0, c.FA) if half == 0 else (c.FA, c.FB)
            for fi in range(nf):
                f = f0 + fi
                w = wload(wgu_d[li][f], 2 * KC * P)
                wv = wring[w][:, 0:2 * KC * P].rearrange("p (g k j) -> p g k j", g=2, k=KC)
                bg = (2 * fi) % 4
                bu = bg + 1
                for g, bank in ((0, bg), (1, bu)):
                    for k in range(KC):
                        mm(psum[bank][:], wv[:, g, k, :], hT[:, k, :], k == 0, k == KC - 1,
                           [r_w[w], r_hT[k]], [r_ps[bank]])
                j = fi % 2
                act(sg_t[j][:], psum[bg][:], AF.Silu, [r_ps[bg]], [r_sg[j]])
                dve(lambda e, j=j, fi=fi, bu=bu: e.tensor_tensor(out=R[:, fi, :], in0=sg_t[j][:],
                                                                 in1=psum[bu][:], op=ALU.mult),
                    [r_sg[j], r_ps[bu]], [r_R[fi]])
            for cc in range(KC):
                w = wload(wd_d[li][cc][:, f0 * P:(f0 + nf) * P], nf * P)
                wv = wring[w][:, 0:nf * P].rearrange("p (f j) -> p f j", f=nf)
                bank = 4 + cc % 2
                for fi in range(nf):
                    mm(psum[bank][:], wv[:, fi, :], R[:, fi, :], fi == 0, fi == nf - 1,
                       [r_w[w], r_R[fi]], [r_ps[bank]])
                dve(lambda e, cc=cc, bank=bank: e.scalar_tensor_tensor(
                    out=xT[:, cc, :], in0=psum[bank][:], scalar=0.5, in1=xT[:, cc, :],
                    op0=ALU.mult, op1=ALU.add), [r_ps[bank], r_xT[cc]], [r_xT[cc]])
                if half == 1 and cc >= 1:
                    stat_chunk(xT[:, cc - 1, :], r_xT[cc - 1], cc - 1, KC, D)
        stat_chunk(xT[:, KC - 1, :], r_xT[KC - 1], KC - 1, KC, D)
        stat_finish(next_ridx)

    NHI = 4 if D % (4 * TT) == 0 else 1
    HWI = D // NHI
    assert HWI <= TT
    NHO = 2 if D % (2 * TT) == 0 else 1
    HWO = D // NHO
    xq_state = {}
    NQ = 4 * NHI
    CPQ = max(1, (HWI * 4) // (TT * 2))
    NHT = max(0, min(8, max(KC, MIXC) // CPQ, NQ - 3))

    def stage_loc(q):
        if q < 3 or q >= 3 + NHT:
            slot = q if q < 3 else (q - 3 - NHT) % 3
            return FS[:, slot * TT:slot * TT + HWI], [r_FS[slot]], "xin%d" % slot
        i = q - 3
        v = hT[:, i * CPQ:(i + 1) * CPQ, :].rearrange("p c t -> p (c t)").bitcast(F32)
        return v[:, 0:HWI], r_hT[i * CPQ:(i + 1) * CPQ], "xin%d" % (3 + i)

    def load_quarters(seq, T, upto):
        if seq >= c.NSEQ:
            return
        tok0 = seq * c.SEQ + T * TT
        st = xq_state.setdefault((seq, T), 0)
        while st < min(upto, NQ):
            tb, hh = st // NHI, st % NHI
            dst, res, key = stage_loc(st)
            S.add("sp", lambda e, tb=tb, hh=hh, dst=dst, tok0=tok0: e.dma_start(
                out=dst, in_=x_d[tok0 + tb * P: tok0 + (tb + 1) * P, hh * HWI:(hh + 1) * HWI]),
                [], res, dma_key=key)
            st += 1
        xq_state[(seq, T)] = st

    def do_tile(seq, T):
        if True:
            tok0 = seq * c.SEQ + T * TT
            gb0 = T * 4

            S.tag = "xin"
            load_quarters(seq, T, 3 + NHT)
            for q in range(NQ):
                tb, hh = q // NHI, q % NHI
                stg, sres, _ = stage_loc(q)
                kbase = hh * (HWI // P)
                for k0 in range(0, HWI // P, 4):
                    nk = min(4, HWI // P - k0)
                    bank = (pctr["i"]) % 4
                    pctr["i"] += 1
                    for kk in range(nk):
                        S.add("pe", lambda e, bank=bank, kk=kk, k0=k0, stg=stg: e.transpose(
                            psum[bank][:, kk * P:(kk + 1) * P], stg[:, (k0 + kk) * P:(k0 + kk + 1) * P],
                            identf[:]), sres + [r_const], [r_ps[bank]])
                    src = psum[bank][:, 0:nk * P].rearrange("p (k t) -> p k t", k=nk)
                    dst = xT[:, kbase + k0:kbase + k0 + nk, tb * P:(tb + 1) * P]
                    if q % 2 == 0:
                        dve(lambda e, dst=dst, src=src: e.tensor_copy(out=dst, in_=src),
                            [r_ps[bank]], r_xT[kbase + k0:kbase + k0 + nk])
                    else:
                        S.add("act", lambda e, dst=dst, src=src: e.copy(out=dst, in_=src),
                              [r_ps[bank]], r_xT[kbase + k0:kbase + k0 + nk])
                if q < 3:
                    load_quarters(seq, T, 3 + NHT + q + 1)
                elif q >= 3 + NHT:
                    load_quarters(seq, T, q + 4)
                if tb == 3 and hh >= 1:
                    pb = (hh - 1) * (HWI // P)
                    for kk in range(HWI // P):
                        stat_chunk(xT[:, pb + kk, :], r_xT[pb + kk], pb + kk, KC, D)
            pb = (NHI - 1) * (HWI // P)
            for kk in range(HWI // P):
                stat_chunk(xT[:, pb + kk, :], r_xT[pb + kk], pb + kk, KC, D)
            stat_finish(0)

            S.tag = "ffn1"
            ffn(0, V_G1, True, 1)

            S.tag = "proj"
            norm_to_h(V_GM, 1, True)
            if T > 0:
                S.add("dve", lambda e: e.tensor_copy(out=ksw[:, :, 0:P], in_=ksw[:, :, 4 * P:5 * P]),
                      [r_ksw[4]], [r_ksw[0]])
                S.add("dve", lambda e: e.tensor_copy(out=vsw[:, 0, :], in_=vsw[:, 4, :]),
                      [r_vsw[4]], [r_vsw[0]])

            def proj_chunk(widx, sub, bank):
                wv = wring[widx][:, 0:2 * KC * P].rearrange("p (n k j) -> p n k j", n=2, k=KC)
                for k in range(KC):
                    mm(psum[bank][:], wv[:, sub, k, :], hT[:, k, :], k == 0, k == KC - 1,
                       [r_w[widx], r_hT[k]], [r_ps[bank]])

            def evac_qpad(bank, chunk):
                for hf in range(2):
                    u = 2 * chunk + hf
                    dve(lambda e, u=u, hf=hf, bank=bank: e.tensor_scalar(
                        out=R[:, u, :], in0=psum[bank][:], scalar1=vecs[:, V_QM + hf:V_QM + hf + 1], scalar2=None,
                        op0=ALU.mult), [r_ps[bank], r_const], [r_R[u]])

            def project_qk(ch_list, handler):
                i = 0
                while i < len(ch_list):
                    ch0 = ch_list[i]
                    n = 2 if (i + 1 < len(ch_list)) else 1
                    if n == 2:
                        w = wload2(wqk_d[ch0:ch0 + 2], KC * P)
                    else:
                        w = wload(wqk_d[ch0], KC * P)
                    for sub in range(n):
                        bank = pctr["i"] % 4
                        pctr["i"] += 1
                        proj_chunk(w, sub, bank)
                        handler(ch_list[i + sub], bank)
                    i += n

            CH_SBQ, CH_SBK, CH_SWQ, CH_SWK = 0, SBC, 2 * SBC, 2 * SBC + SWC

            def h_swq(ch, bank):
                evac_qpad(bank, ch - CH_SWQ)

            def h_swk(ch, bank):
                kcid = ch - CH_SWK
                act(ksw[:, kcid, P:5 * P], psum[bank][:], AF.Copy, [r_ps[bank]], r_ksw[1:5])

            project_qk(list(range(CH_SWQ, CH_SWQ + SWC)), h_swq)
            project_qk(list(range(CH_SWK, CH_SWK + KVC)), h_swk)
            w = wload(wva_d, KC * c.SWV)
            wv = wring[w][:, 0:KC * c.SWV].rearrange("p (k n) -> p k n", k=KC)
            for tb in range(4):
                bank = pctr["i"] % 4
                pctr["i"] += 1
                for k in range(KC):
                    mm(psum[bank][:, 0:c.SWV], hT[:, k, tb * P:(tb + 1) * P], wv[:, k, :], k == 0, k == KC - 1,
                       [r_w[w], r_hT[k]], [r_ps[bank]])
                dve(lambda e, tb=tb, bank=bank: e.tensor_copy(out=vsw[:, 1 + tb, :], in_=psum[bank][:, 0:c.SWV]),
                    [r_ps[bank]], [r_vsw[1 + tb]])

            def h_sbk(ch, bank):
                j = ch - CH_SBK
                S.add("act", lambda e, j=j, bank=bank: e.copy(out=kc_sb[:, j, T * TT:(T + 1) * TT], in_=psum[bank][:]),
                      [r_ps[bank]], [r_kc[j][T]])

            project_qk(list(range(CH_SBK, CH_SBK + SBC)), h_sbk)
            for vg in range(c.NVG):
                w = wload(wvb_d[vg], KC * c.VG)
                wv = wring[w][:, 0:KC * c.VG].rearrange("p (k n) -> p k n", k=KC)
                for tb in range(4):
                    bank = pctr["i"] % 4
                    pctr["i"] += 1
                    for k in range(KC):
                        mm(psum[bank][:, 0:c.VG], hT[:, k, tb * P:(tb + 1) * P], wv[:, k, :], k == 0, k == KC - 1,
                           [r_w[w], r_hT[k]], [r_ps[bank]])
                    dve(lambda e, tb=tb, bank=bank, vg=vg: e.tensor_copy(
                        out=vc_sb[:, gb0 + tb, vg * c.VG:(vg + 1) * c.VG], in_=psum[bank][:, 0:c.VG]),
                        [r_ps[bank]], [r_vc[gb0 + tb]])

            S.tag = "swa"
            def swa_stage1(u):
                ch, hf = u // 2, u % 2
                kcid = ch // 4
                sbi = u % 2
                S.add("sp", lambda e, u=u, sbi=sbi: e.dma_start(out=sbias_t[sbi][:], in_=sbias_d[u]),
                      [], [r_sbias[sbi]], dma_key="sbias%d" % sbi)
                for bk in range(2):
                    zb = (u % 2) * 2 + bk
                    for q2 in range(2):
                        qb = bk * 2 + q2
                        first = (T == 0 and qb == 0)
                        base = q2 * 2 * P
                        started = False
                        for part in range(2):
                            if first and part == 0:
                                continue
                            kslot = qb + part
                            mm(psum[zb][:, base + part * P:base + (part + 1) * P],
                               ksw[:, kcid, kslot * P:(kslot + 1) * P], R[:, u, qb * P:(qb + 1) * P],
                               not started, True, [r_ksw[kslot], r_R[u]], [r_ps[zb]])
                            started = True
                        c0 = P if first else 0
                        for hl_ in range(2):
                            mm(psum[zb][:, base + c0:base + 2 * P], identb, sbias_t[sbi][:, hl_, c0:2 * P],
                               False, True, [r_sbias[sbi], r_const], [r_ps[zb]])
                    a0 = P if (T == 0 and bk == 0) else 0
                    pi = (u % 2) * 2 + bk
                    act(At[pi][:, a0:TT], psum[zb][:, a0:TT], AF.Exp, [r_ps[zb]], [r_A[pi]])

            def swa_stage2(u):
                ch, hf = u // 2, u % 2
                kcid = ch // 4
                lo, hi = hf * HD, (hf + 1) * HD
                ob, db = 4 + (u % 2) * 2, 5 + (u % 2) * 2
                for qb in range(4):
                    first = (T == 0 and qb == 0)
                    pi = (u % 2) * 2 + qb // 2
                    base = (qb % 2) * 2 * P
                    parts = [1] if first else [0, 1]
                    for n, part in enumerate(parts):
                        kslot = qb + part
                        mm(psum[ob][:, qb * P:(qb + 1) * P], vsw[:, kslot, kcid * P:(kcid + 1) * P],
                           At[pi][:, base + part * P:base + (part + 1) * P], n == 0, n == len(parts) - 1,
                           [r_vsw[kslot], r_A[pi]], [r_ps[ob]])
                    for n, part in enumerate(parts):
                        mm(psum[db][:, qb * P:(qb + 1) * P], onesb, At[pi][:, base + part * P:base + (part + 1) * P],
                           n == 0, n == len(parts) - 1, [r_A[pi], r_const], [r_ps[db]])
                fi = 2 + (u % 2)
                dn = FS[:, fi * TT:(fi + 1) * TT]
                act(dn, psum[db][:], AF.Ln, [r_ps[db], r_esink], [r_FS[fi]], bias=esink[:, u:u + 1])
                act(dn, dn, AF.Exp, [r_FS[fi]], [r_FS[fi]], scale=-1.0)
                dve(lambda e, ob=ob, dn=dn, lo=lo, hi=hi, ch=ch: e.tensor_tensor(
                    out=R[lo:hi, c.YA0 + ch, :], in0=psum[ob][lo:hi, :], in1=dn[lo:hi, :], op=ALU.mult),
                    [r_ps[ob], r_FS[fi]], [r_R[c.YA0 + ch]])

            swa_stage1(0)
            for u in range(c.NSWA):
                if u + 1 < c.NSWA:
                    swa_stage1(u + 1)
                swa_stage2(u)

            rmsnorm_stats(lambda k: ychunk(k), [yres(k) for k in range(SWC)], SWC, SWC * P, 0)
            for k in range(SWC):
                dve(lambda e, k=k: e.scalar_tensor_tensor(
                    out=ychunk(k), in0=ychunk(k), scalar=vecs[:, V_GYA + k:V_GYA + k + 1],
                    in1=rstd_t[0][:], op0=ALU.mult, op1=ALU.mult),
                    [yres(k), r_rstd[0], r_const], [yres(k)])

            S.tag = "sbq"
            def h_sbq(ch, bank):
                evac_qpad(bank, ch - CH_SBQ)

            n_up = min(2, SBC)
            project_qk(list(range(CH_SBQ, CH_SBQ + n_up)), h_sbq)
            weave = []

            def make_weave():
                for ch0 in range(n_up, SBC, 2):
                    n = min(2, SBC - ch0)
                    st = {}

                    def ld(ch0=ch0, n=n, st=st):
                        if n == 2:
                            st["w"] = wload2(wqk_d[CH_SBQ + ch0:CH_SBQ + ch0 + 2], KC * P)
                        else:
                            st["w"] = wload(wqk_d[CH_SBQ + ch0], KC * P)
                    weave.append((ch0, ld))
                    for sub in range(n):
                        for k in range(KC):
                            def one(sub=sub, k=k, st=st):
                                wv = wring[st["w"]][:, 0:2 * KC * P].rearrange("p (n k j) -> p n k j", n=2, k=KC)
                                mm(psum[6][:], wv[:, sub, k, :], hT[:, k, :], k == 0, k == KC - 1,
                                   [r_w[st["w"]], r_hT[k]], [r_ps[6]])
                            weave.append((ch0 + sub, one))
                        weave.append((ch0 + sub, lambda ch=ch0 + sub: evac_qpad(6, ch)))

            make_weave()
            BIG = 10 ** 6
            for cc in range(KC):
                st = {}

                def ldo(cc=cc, st=st):
                    st["w"] = wload(wo_d[cc][:, 0:SWC * P], SWC * P)
                weave.append((BIG, ldo))
                for m in range(SWC):
                    def one(cc=cc, m=m, st=st):
                        wv = wring[st["w"]][:, 0:SWC * P].rearrange("p (m j) -> p m j", m=SWC)
                        mm(psum[6][:], wv[:, m, :], ychunk(m), m == 0, m == SWC - 1,
                           [r_w[st["w"]], yres(m)], [r_ps[6]])
                    weave.append((BIG, one))

                def addx(cc=cc):
                    dve(lambda e: e.tensor_tensor(out=xT[:, cc, :], in0=psum[6][:], in1=xT[:, cc, :], op=ALU.add),
                        [r_ps[6], r_xT[cc]], [r_xT[cc]])
                weave.append((BIG, addx))
            wpos = {"i": 0}

            def weave_step(nmax=1):
                for _ in range(nmax):
                    if wpos["i"] < len(weave):
                        weave[wpos["i"]][1]()
                        wpos["i"] += 1

            def weave_flush(upto_chunk):
                while wpos["i"] < len(weave) and weave[wpos["i"]][0] <= upto_chunk:
                    weave[wpos["i"]][1]()
                    wpos["i"] += 1

            nblk_here = gb0 + 4

            def sb_head_steps(h, hl):
                j, hf = h // 2, h % 2
                zbanks = [hl * 4 + 0, hl * 4 + 1]
                cbank = 2
                obank = hl * 4 + 3
                blocks = list(range(nblk_here - 1, -1, -1))
                steps = []

                def colrange(b):
                    r = b - gb0
                    return (r * P if r > 0 else 0), TT

                def stageA(i):
                    b = blocks[i]
                    zb = zbanks[i % 2]
                    c0, c1 = colrange(b)
                    kT = kc_sb[:, j, b * P:(b + 1) * P]
                    mm(psum[zb][:, c0:c1], kT, R[:, h, c0:c1], True, True,
                       [r_kc[j][b // 4], r_R[h]], [r_ps[zb]])
                    if b >= gb0:
                        mm(psum[zb][:, c0:c0 + P], identb, negmask, False, True, [r_const], [r_ps[zb]])
                    E = FS[:, hl * TT:(hl + 1) * TT]
                    act(E[:, c0:c1], psum[zb][:, c0:c1], AF.Exp, [r_ps[zb]], [r_FS[hl]])
                    si = hl * 2 + i % 2
                    act(SPt[si][:, c0:c1], E[:, c0:c1], AF.Ln, [r_FS[hl]], [r_SP[si]], bias=1.0)

                def stageB(i):
                    b = blocks[i]
                    zb = zbanks[i % 2]
                    c0, c1 = colrange(b)
                    si = hl * 2 + i % 2
                    mm(psum[zb][:, c0:c1], triNeg, SPt[si][:, c0:c1], False, True,
                       [r_SP[si], r_const], [r_ps[zb]])
                    if i > 0:
                        p0, p1 = colrange(blocks[i - 1])
                        mm(psum[zb][:, p0:p1], sel0neg, CRt[hl][:, p0:p1], False, True,
                           [r_CR[hl], r_const], [r_ps[zb]])
                    act(At[si][:, c0:c1], psum[zb][:, c0:c1], AF.Exp, [r_ps[zb]], [r_A[si]])
                    if i < len(blocks) - 1:
                        mm(psum[cbank][:, c0:c1], onesb, SPt[si][:, c0:c1], True, True,
                           [r_SP[si], r_const], [r_ps[cbank]])
                        q0 = colrange(blocks[i - 1])[0] if i > 0 else c1
                        if q0 > c0:
                            dve(lambda e: e.tensor_copy(out=CRt[hl][:, c0:q0], in_=psum[cbank][:, c0:q0]),
                                [r_ps[cbank]], [r_CR[hl]])
                        if q0 < c1:
                            dve(lambda e: e.tensor_tensor(out=CRt[hl][:, q0:c1], in0=CRt[hl][:, q0:c1],
                                                          in1=psum[cbank][:, q0:c1], op=ALU.add),
                                [r_ps[cbank], r_CR[hl]], [r_CR[hl]])

                def stageC(i):
                    b = blocks[i]
                    c0, c1 = colrange(b)
                    si = hl * 2 + i % 2
                    mm(psum[obank][:, c0:c1], vc_sb[:, b, j * P:(j + 1) * P], At[si][:, c0:c1],
                       i == 0, i == len(blocks) - 1, [r_vc[b], r_A[si]], [r_ps[obank]])

                def fin():
                    lo, hi = hf * HD, (hf + 1) * HD
                    dve(lambda e: e.tensor_copy(out=R[lo:hi, 2 * j, :], in_=psum[obank][lo:hi, :]),
                        [r_ps[obank]], [r_R[2 * j]])

                nb = len(blocks)
                for k in range(nb + 2):
                    def cyc(k=k):
                        if k < nb:
                            stageA(k)
                        if 0 <= k - 1 < nb:
                            stageB(k - 1)
                        if 0 <= k - 2 < nb:
                            stageC(k - 2)
                    steps.append(cyc)
                steps.append(fin)
                return steps

            S.tag = "sb"
            steps_left = (c.NSB // 2) * 2 * (nblk_here + 3)
            for hp in range(c.NSB // 2):
                weave_flush(hp)
                sa = sb_head_steps(2 * hp, 0)
                sb_ = sb_head_steps(2 * hp + 1, 1)
                for a, b_ in zip(sa, sb_):
                    for fn in (a, b_):
                        fn()
                        rem = len(weave) - wpos["i"]
                        weave_step(-(-rem // max(1, steps_left)))
                        steps_left -= 1
            weave_flush(BIG)

            nseq, nT = (seq, T + 1) if T + 1 < NT else (seq + 1, 0)
            load_quarters(nseq, nT, 3)
            S.tag = "wout"
            for grp, (m0, ncn, gcol) in ((1, (SWC, SBC, V_GYB)),):
                rmsnorm_stats(lambda k, m0=m0: ychunk(m0 + k), [yres(m0 + k) for k in range(ncn)], ncn, ncn * P, grp)
                for k in range(ncn):
                    dve(lambda e, k=k, m0=m0, gcol=gcol, grp=grp: e.scalar_tensor_tensor(
                        out=ychunk(m0 + k), in0=ychunk(m0 + k), scalar=vecs[:, gcol + k:gcol + k + 1],
                        in1=rstd_t[grp][:], op0=ALU.mult, op1=ALU.mult),
                        [yres(m0 + k), r_rstd[grp], r_const], [yres(m0 + k)])
            for cc in range(KC):
                w = wload(wo_d[cc][:, SWC * P:MIXC * P], SBC * P)
                wv = wring[w][:, 0:SBC * P].rearrange("p (m j) -> p m j", m=SBC)
                bank = 4 + cc % 2
                for m in range(SBC):
                    mm(psum[bank][:], wv[:, m, :], ychunk(SWC + m), m == 0, m == SBC - 1,
                       [r_w[w], yres(SWC + m)], [r_ps[bank]])
                dve(lambda e, cc=cc, bank=bank: e.tensor_tensor(out=xT[:, cc, :], in0=psum[bank][:],
                                                                in1=xT[:, cc, :], op=ALU.add),
                    [r_ps[bank], r_xT[cc]], [r_xT[cc]])
                if cc >= 1:
                    stat_chunk(xT[:, cc - 1, :], r_xT[cc - 1], cc - 1, KC, D)
            stat_chunk(xT[:, KC - 1, :], r_xT[KC - 1], KC - 1, KC, D)
            stat_finish(0)

            S.tag = "ffn2"
            ffn(1, V_G3, True, 0)
            load_quarters(nseq, nT, 3 + NHT)

            S.tag = "fin"
            for k in range(KC):
                dve(lambda e, k=k: e.scalar_tensor_tensor(out=xT[:, k, :], in0=xT[:, k, :],
                                                          scalar=vecs[:, V_GF + k:V_GF + k + 1],
                                                          in1=rstd_t[0][:], op0=ALU.mult, op1=ALU.mult),
                    [r_xT[k], r_rstd[0], r_const], [r_xT[k]])
            hqo = (HWO + TT - 1) // TT
            for tb in range(4):
                for hh in range(NHO):
                    fres = r_FS[3 + hh * hqo:3 + (hh + 1) * hqo]
                    fo = (3 + hh * hqo) * TT
                    kbase = hh * (HWO // P)
                    for k0 in range(0, HWO // P, 4):
                        nk = min(4, HWO // P - k0)
                        bank = pctr["i"] % 4
                        pctr["i"] += 1
                        for kk in range(nk):
                            k = kbase + k0 + kk
                            S.add("pe", lambda e, bank=bank, kk=kk, k=k, tb=tb: e.transpose(
                                psum[bank][:, kk * P:(kk + 1) * P], xT[:, k, tb * P:(tb + 1) * P], identf[:]),
                                [r_xT[k], r_const], [r_ps[bank]])
                        dst = FS[:, fo + k0 * P:fo + (k0 + nk) * P]
                        src = psum[bank][:, 0:nk * P]
                        if (k0 // 4) % 2 == 0:
                            dve(lambda e, dst=dst, src=src: e.tensor_copy(out=dst, in_=src), [r_ps[bank]], fres)
                        else:
                            S.add("act", lambda e, dst=dst, src=src: e.copy(out=dst, in_=src), [r_ps[bank]], fres)
                    S.add("sp", lambda e, tb=tb, hh=hh, fo=fo: e.dma_start(
                        out=out_d[tok0 + tb * P: tok0 + (tb + 1) * P, hh * HWO:(hh + 1) * HWO], in_=FS[:, fo:fo + HWO]),
                        fres, [r_out[hh]], dma_key="out%d" % hh)

    for seq in range(c.NSEQ):
        for T in range(NT):
            do_tile(seq, T)

    S.add("sp", lambda e: None, r_out, [])

    cnt, dcnt = S.finalize()
    nc._sched = S
    sem_names = list(Sched.ENGS) + list(dcnt.keys())
    sems = {n: es.enter_context(nc.semaphore("s_" + n)) for n in sem_names}
    with nc.Block() as block:
        @block.tensor
        def _(e):
            S.emit("pe", e, sems)

        @block.scalar
        def _(e):
            S.emit("act", e, sems)

        @block.vector
        def _(e):
            S.emit("dve", e, sems)

        @block.gpsimd
        def _(e):
            S.emit("pool", e, sems)

        @block.sync
        def _(e):
            S.emit("sp", e, sems)
    es.close()
    return nc


def _stat(W, nm):
    K, N = W.shape
    kc, m = K // P, N // P
    return np.ascontiguousarray(W.reshape(kc, P, m, P).transpose(2, 1, 0, 3).reshape(m, P, kc * P))


def _mov(W):
    K, N = W.shape
    kc = K // P
    return np.ascontiguousarray(W.reshape(kc, P, N).transpose(1, 0, 2).reshape(P, kc * N))


def _vec(v):
    return np.ascontiguousarray(v.reshape(-1, P).T)


def prepare_inputs(cfg, x, ffn1_norm, ffn1_w_gate, ffn1_w_up, ffn1_w_down, mix_norm, w_in, swa_sinks,
                   swa_out_norm, sb_out_norm, w_out, ffn2_norm, ffn2_w_gate, ffn2_w_up, ffn2_w_down, final_norm):
    c = cfg
    f32 = np.float32
    sh = {}

    def gu(wg, wu):
        g = _stat(np.asarray(wg[0], f32), "g")
        u = _stat(np.asarray(wu[0], f32), "u")
        return np.ascontiguousarray(np.concatenate([g, u], axis=2))

    sh["wgu1"] = gu(ffn1_w_gate, ffn1_w_up)
    sh["wgu2"] = gu(ffn2_w_gate, ffn2_w_up)
    sh["wd1"] = _stat(np.asarray(ffn1_w_down[0], f32), "d")
    sh["wd2"] = _stat(np.asarray(ffn2_w_down[0], f32), "d")
    W = np.asarray(w_in[0], f32)
    o1 = c.NSWA * HD
    o2 = o1 + c.NKV * HD
    o3 = o2 + c.NKV * HD
    o4 = o3 + c.NSB * HD
    o5 = o4 + c.NSB * HD
    swa_heads = [swa_slot_to_head(c, u) for u in range(c.NSWA)]
    swq_cols = np.concatenate([np.arange(h * HD, (h + 1) * HD) for h in swa_heads])
    qk_cols = np.concatenate([np.arange(o3, o4), np.arange(o4, o5), swq_cols, np.arange(o1, o2)])
    sh["wqk"] = _stat(W[:, qk_cols], "qk")
    wvb = W[:, o5:]
    sh["wvb"] = np.ascontiguousarray(np.stack([_mov(wvb[:, g * c.VG:(g + 1) * c.VG]) for g in range(c.NVG)]))
    sh["wva"] = _mov(W[:, o2:o3])
    Wo = np.asarray(w_out[0], f32)
    rows = np.concatenate([swq_cols, np.arange(o1, o1 + c.NSB * HD)])
    sh["wo"] = _stat(Wo[rows, :], "o")
    gya = np.asarray(swa_out_norm[0], f32)[swq_cols]
    sinks = np.asarray(swa_sinks[0], f32)[swa_heads]
    vec = np.concatenate([_vec(np.asarray(ffn1_norm[0], f32)), _vec(np.asarray(mix_norm[0], f32)),
                          _vec(np.asarray(ffn2_norm[0], f32)), _vec(np.asarray(final_norm, f32)),
                          _vec(gya), _vec(np.asarray(sb_out_norm[0], f32)),
                          np.broadcast_to(sinks[None, :], (P, c.NSWA)),
                          np.stack([np.where(np.arange(P) < HD, HD ** -0.5, 0.0), np.where(np.arange(P) >= HD, HD ** -0.5, 0.0),
                                    (np.arange(P) == P - 1).astype(f32)], axis=1).astype(f32)], axis=1)
    sh["vecs"] = np.ascontiguousarray(vec, dtype=f32)
    sh["cf32"] = np.eye(P, dtype=f32)
    idx = np.arange(P)
    tri = -(idx[:, None] >= idx[None, :]).astype(f32)
    sel0 = np.zeros((P, P), f32)
    sel0[0, :] = -1.0
    negm = np.where(idx[:, None] < idx[None, :], 0.0, NEG).astype(f32)
    sh["cbf"] = np.ascontiguousarray(np.concatenate([np.eye(P, dtype=f32), tri, np.ones((P, P), f32), sel0, negm], axis=1))
    import ml_dtypes
    bf = ml_dtypes.bfloat16
    sb_ = np.empty((c.NSWA, P, 2, 2 * P), bf)
    kpos = np.concatenate([idx, idx + P])
    qpos = idx + P
    dist = qpos[None, :] - kpos[:, None]
    valid = (dist >= 0) & (dist < P)
    for u, h in enumerate(swa_heads):
        slope = np.float32(2.0) ** np.float32(-8.0 * (h + 1) / c.NSWA)
        b = np.where(valid, -(slope * dist.astype(f32)), NEG).astype(f32)
        b2 = np.concatenate([b[0:P, :], b[P:2 * P, :]], axis=1)
        hi_ = b2.astype(bf)
        lo_ = (b2 - hi_.astype(f32)).astype(bf)
        sb_[u, :, 0, :] = hi_
        sb_[u, :, 1, :] = lo_
    sh["sbias"] = sb_
    xs = np.asarray(x, f32).reshape(c.NCORES, c.NSEQ * c.SEQ, c.D)
    return sh, xs


def run(cfg, inputs):
    sh, xs = prepare_inputs(cfg, **inputs)
    nc = build_program(cfg)
    in_maps = []
    for i in range(cfg.NCORES):
        m = dict(sh)
        m["x"] = np.ascontiguousarray(xs[i])
        in_maps.append(m)
    res = run_bass_kernel_spmd(nc, in_maps, core_ids=list(range(cfg.NCORES)))
    out = np.stack([np.asarray(r["out"]) for r in res.results], axis=0)
    return out


def kernel(x, ffn1_norm, ffn1_w_gate, ffn1_w_up, ffn1_w_down, mix_norm, w_in, swa_sinks, swa_out_norm,
           sb_out_norm, w_out, ffn2_norm, ffn2_w_gate, ffn2_w_up, ffn2_w_down, final_norm):
    cfg = Cfg()
    B, Sq, Dm = x.shape
    out = run(cfg, dict(x=x, ffn1_norm=ffn1_norm, ffn1_w_gate=ffn1_w_gate, ffn1_w_up=ffn1_w_up,
                        ffn1_w_down=ffn1_w_down, mix_norm=mix_norm, w_in=w_in, swa_sinks=swa_sinks,
                        swa_out_norm=swa_out_norm, sb_out_norm=sb_out_norm, w_out=w_out, ffn2_norm=ffn2_norm,
                        ffn2_w_gate=ffn2_w_gate, ffn2_w_up=ffn2_w_up, ffn2_w_down=ffn2_w_down,
                        final_norm=final_norm))
    return out.reshape(B, Sq, Dm).astype(np.float32)
```
